# Optimizing a Trainium2 kernel written in Bass

```python
import jax, jax.numpy as jnp
from jax import lax
import numpy as np

D_MODEL = 2048
BATCH = 1
SEQ = 8192
DEPTH = 4

CHUNK = 64
LEFT_CHUNKS = 8
BAND = (LEFT_CHUNKS + 1) * CHUNK
MAX_REL = 128
N_MEM = 256

MIX_WIDTH = D_MODEL
A_WIDTH = MIX_WIDTH // 2
A_HEADS = 8
A_HEAD_DIM = A_WIDTH // A_HEADS
B_WIDTH = MIX_WIDTH // 4
B_HEADS = 4
B_DV = B_WIDTH // B_HEADS
B_DK = B_DV // 2
B_KEY_WIDTH = B_HEADS * B_DK
GATE_RANK = 16
GATE_TAU = 16.0
M_WIDTH = MIX_WIDTH // 4
M_HEADS = 4
M_HEAD_DIM = M_WIDTH // M_HEADS

IN_SPLITS = (A_WIDTH, A_WIDTH, A_WIDTH, A_WIDTH,
             B_KEY_WIDTH, B_KEY_WIDTH, B_WIDTH, B_WIDTH, GATE_RANK,
             M_WIDTH, M_WIDTH)
IN_WIDTH = sum(IN_SPLITS)
VALUE_SPLITS = (2, 6)

DEEPNORM_ALPHA = (2.0 * DEPTH) ** 0.25
DEEPNORM_BETA = (8.0 * DEPTH) ** -0.25
LN_EPS = 1e-5
RMS_EPS = 1e-6
NEG_INF = -1e30

kernel_name = "hybrid_chunk_attn_gla_mem_deepnorm"


def layer_norm(x, g, b):
    xf = x.astype(jnp.float32)
    mu = jnp.mean(xf, axis=-1, keepdims=True)
    var = jnp.mean(jnp.square(xf - mu), axis=-1, keepdims=True)
    y = (xf - mu) * lax.rsqrt(var + LN_EPS) * g.astype(jnp.float32) + b.astype(jnp.float32)
    return y.astype(x.dtype)


def chunk_band_attention(q, k, v, rel_table):
    B, S, H, Dh = q.shape
    nc = S // CHUNK
    qc = q.reshape(B, nc, CHUNK, H, Dh)
    pad = ((0, 0), (LEFT_CHUNKS * CHUNK, 0), (0, 0), (0, 0))
    kp = jnp.pad(k, pad).reshape(B, nc + LEFT_CHUNKS, CHUNK, H, Dh)
    vp = jnp.pad(v, pad).reshape(B, nc + LEFT_CHUNKS, CHUNK, H, Dh)
    k_band = jnp.concatenate([kp[:, i:i + nc] for i in range(LEFT_CHUNKS + 1)], axis=2)
    v_band = jnp.concatenate([vp[:, i:i + nc] for i in range(LEFT_CHUNKS + 1)], axis=2)
    scores = jnp.einsum('bnqhd,bnkhd->bnhqk', qc, k_band).astype(jnp.float32) * (Dh ** -0.5)
    dist = jnp.arange(CHUNK)[:, None] + LEFT_CHUNKS * CHUNK - jnp.arange(BAND)[None, :]
    rel_idx = jnp.clip(dist, -MAX_REL, MAX_REL) + MAX_REL
    bias = rel_table[:, rel_idx].astype(jnp.float32)
    key_chunk = jnp.arange(nc)[:, None] - LEFT_CHUNKS + (jnp.arange(BAND) // CHUNK)[None, :]
    valid = key_chunk >= 0
    scores = jnp.where(valid[None, :, None, None, :], scores + bias[None, None], NEG_INF)
    p = jax.nn.softmax(scores, axis=-1).astype(v.dtype)
    out = jnp.einsum('bnhqk,bnkhd->bnqhd', p, v_band)
    return out.reshape(B, S, H, Dh)


def gla_chunk_recurrence(q, k, v, log_g):
    B, S, H, DK = q.shape
    DV = v.shape[-1]
    nc = S // CHUNK
    f32 = jnp.float32

    def to_chunks(t):
        return jnp.moveaxis(t.astype(f32).reshape(B, nc, CHUNK, H, t.shape[-1]), 1, 0)

    qs = to_chunks(q * (DK ** -0.5))
    ks, vs, gs = to_chunks(k), to_chunks(v), to_chunks(log_g)

    def step(state, inp):
        qc, kc, vc, gc = inp
        b = jnp.cumsum(gc, axis=1)
        decay = jnp.exp(-jnp.abs(b[:, :, None] - b[:, None, :]))
        attn = jnp.einsum('bihd,bjhd,bijhd->bhij', qc, kc, decay)
        o_intra = jnp.einsum('bhij,bjhv->bihv', attn, vc)
        o_inter = jnp.einsum('bihd,bhdv->bihv', qc * jnp.exp(b), state)
        b_last = b[:, -1]
        k_dec = kc * jnp.exp(b_last[:, None] - b)
        new_state = jnp.exp(b_last)[..., None] * state + jnp.einsum('bjhd,bjhv->bhdv', k_dec, vc)
        return new_state, o_intra + o_inter

    state0 = jnp.zeros((B, H, DK, DV), f32)
    _, out = lax.scan(step, state0, (qs, ks, vs, gs))
    return jnp.moveaxis(out, 0, 1).reshape(B, S, H, DV)


def memory_attention(q, mk, mv):
    s = jnp.einsum('bshd,bmhd->bhsm', q, mk).astype(jnp.float32) * (q.shape[-1] ** -0.5)
    p = jax.nn.softmax(s, axis=-1).astype(mv.dtype)
    return jnp.einsum('bhsm,bmhd->bshd', p, mv)


def hybrid_layer(x, mem, w_in, rel_table, gate_w, gate_b, gla_norm_g, w_mem_kv, w_out, ln_g, ln_b):
    B, S, _ = x.shape
    h = x @ w_in
    cuts = [int(c) for c in np.cumsum(IN_SPLITS)[:-1]]
    a_q, a_k, a_v, a_z, b_q, b_k, b_v, b_z, b_lr, m_q, m_z = jnp.split(h, cuts, axis=-1)

    heads_a = lambda t: t.reshape(B, S, A_HEADS, A_HEAD_DIM)
    y_a = chunk_band_attention(heads_a(a_q), heads_a(a_k), heads_a(a_v), rel_table).reshape(B, S, A_WIDTH)

    gate_logit = (b_lr @ gate_w + gate_b).astype(jnp.float32)
    log_g = jax.nn.log_sigmoid(gate_logit) / GATE_TAU
    heads_k = lambda t: t.reshape(B, S, B_HEADS, B_DK)
    o_b = gla_chunk_recurrence(heads_k(b_q), heads_k(b_k), b_v.reshape(B, S, B_HEADS, B_DV), heads_k(log_g))
    o_b = o_b * lax.rsqrt(jnp.mean(jnp.square(o_b), axis=-1, keepdims=True) + RMS_EPS) * gla_norm_g.astype(jnp.float32)
    y_b = o_b.reshape(B, S, B_WIDTH).astype(x.dtype)

    mkv = mem @ w_mem_kv
    mk, mv = jnp.split(mkv, 2, axis=-1)
    heads_m = lambda t: t.reshape(t.shape[0], t.shape[1], M_HEADS, M_HEAD_DIM)
    y_m = memory_attention(heads_m(m_q), heads_m(mk), heads_m(mv)).reshape(B, S, M_WIDTH)

    y = jnp.concatenate([y_a * jax.nn.silu(a_z), y_b * jax.nn.silu(b_z), y_m * jax.nn.silu(m_z)], axis=-1)
    out = y @ w_out
    return layer_norm(DEEPNORM_ALPHA * x + out, ln_g, ln_b)


def setup_inputs(seed: int = 0) -> dict:
    key = jax.random.key(seed)
    ks = jax.random.split(key, 12)
    f32 = jnp.float32
    x = jax.random.normal(ks[0], (BATCH, SEQ, D_MODEL), f32)
    mem = jax.random.normal(ks[1], (N_MEM, D_MODEL), f32)[None].repeat(BATCH, axis=0) \
        + 0.1 * jax.random.normal(ks[2], (BATCH, N_MEM, D_MODEL), f32)
    col_scale = jnp.concatenate([
        jnp.full((n,), DEEPNORM_BETA if i in VALUE_SPLITS else 1.0, f32) for i, n in enumerate(IN_SPLITS)])
    w_in = jax.random.normal(ks[3], (DEPTH, D_MODEL, IN_WIDTH), f32) * (D_MODEL ** -0.5) * col_scale
    a_rel_bias = 0.1 * jax.random.normal(ks[4], (DEPTH, A_HEADS, 2 * MAX_REL + 1), f32)
    b_gate_w = jax.random.normal(ks[5], (DEPTH, GATE_RANK, B_KEY_WIDTH), f32) * (GATE_RANK ** -0.5)
    b_gate_b = 0.1 * jax.random.normal(ks[6], (DEPTH, B_KEY_WIDTH), f32)
    b_norm_g = 1.0 + 0.02 * jax.random.normal(ks[7], (DEPTH, B_DV), f32)
    kv_scale = jnp.concatenate([jnp.ones((M_WIDTH,), f32), jnp.full((M_WIDTH,), DEEPNORM_BETA, f32)])
    w_mem_kv = jax.random.normal(ks[8], (DEPTH, D_MODEL, 2 * M_WIDTH), f32) * (D_MODEL ** -0.5) * kv_scale
    w_out = jax.random.normal(ks[9], (DEPTH, MIX_WIDTH, D_MODEL), f32) * (MIX_WIDTH ** -0.5) * DEEPNORM_BETA
    ln_g = 1.0 + 0.02 * jax.random.normal(ks[10], (DEPTH, D_MODEL), f32)
    ln_b = 0.02 * jax.random.normal(ks[11], (DEPTH, D_MODEL), f32)
    return {"x": x, "mem": mem, "w_in": w_in, "a_rel_bias": a_rel_bias, "b_gate_w": b_gate_w,
            "b_gate_b": b_gate_b, "b_norm_g": b_norm_g, "w_mem_kv": w_mem_kv, "w_out": w_out,
            "ln_g": ln_g, "ln_b": ln_b}


def reference(x, mem, w_in, a_rel_bias, b_gate_w, b_gate_b, b_norm_g, w_mem_kv, w_out, ln_g, ln_b):
    h = x
    for l in range(DEPTH):
        h = hybrid_layer(h, mem, w_in[l], a_rel_bias[l], b_gate_w[l], b_gate_b[l], b_norm_g[l],
                         w_mem_kv[l], w_out[l], ln_g[l], ln_b[l])
    return h
```

```python
import numpy as np
import concourse.bass as bass
import concourse.mybir as mybir
from concourse.bass_utils import run_bass_kernel_spmd

F32 = mybir.dt.float32
BF16 = mybir.dt.bfloat16
AF = mybir.ActivationFunctionType
ALU = mybir.AluOpType

NCORES = 8
D = 2048
SEQ = 8192
DEPTH = 4
T = 1024
HALO = 512
TT = T + HALO
KT = 16
INW = 6672
C_AQ, C_AK, C_AV, C_AZ = 0, 1024, 2048, 3072
C_BQ, C_BK, C_BV, C_BZ, C_BLR = 4096, 4352, 4608, 5120, 5632
C_MQ, C_MZ = 5648, 6160
NMEM = 256
NEG = -30000.0
LN_EPS = 1e-5
RMS_EPS = 1e-6
ALPHA = (2.0 * DEPTH) ** 0.25
FUSED = False
import os
GLA_STOP = int(os.environ.get('GLA_STOP', '0'))


class Buf:
    __slots__ = ("name", "w", "r", "dsem", "dcnt")

    def __init__(self, name):
        self.name = name
        self.w = None
        self.r = []
        self.dsem = None
        self.dcnt = 0


class Prog:
    CE = ("pe", "act", "dve", "pool")

    def __init__(self, nc):
        self.nc = nc
        self.lists = {e: [] for e in ("pe", "act", "dve", "pool", "sp")}
        self.cnt = {e: 0 for e in self.CE}
        self.sem = {e: nc.alloc_semaphore(f"sem_{e}") for e in self.CE}
        self.known = {e: {} for e in self.lists}
        self.nd = 0
        self.dbufs = []

    def _deps(self, eng, reads, writes):
        deps = []
        for b in reads:
            if b.w is not None:
                deps.append(b.w)
        for b in writes:
            if b.w is not None:
                deps.append(b.w)
            deps.extend(b.r)
        waits = []
        kn = self.known[eng]
        for dep in deps:
            kind, who, need = dep
            if kind == "e" and who == "pe" and eng == "pe":
                continue
            key = (kind, who if kind == "e" else id(who))
            if kn.get(key, 0) >= need:
                continue
            kn[key] = need
            waits.append((self.sem[who] if kind == "e" else who.dsem, need))
        return waits

    def op(self, eng, fn, reads=(), writes=()):
        waits = self._deps(eng, reads, writes)
        self.cnt[eng] += 1
        tok = ("e", eng, self.cnt[eng])
        for b in reads:
            b.r.append(tok)
        for b in writes:
            b.w = tok
            b.r = []
        self.lists[eng].append((waits, fn, self.sem[eng], 1))

    def dma(self, q, fn, dbuf, reads=(), writes=(), inc=16):
        waits = self._deps(q, reads, writes)
        if dbuf.dsem is None:
            dbuf.dsem = self.nc.alloc_semaphore(f"dsem{self.nd}")
            self.dbufs.append(dbuf)
            self.nd += 1
        dbuf.dcnt += inc
        tok = ("d", dbuf, dbuf.dcnt)
        for b in reads:
            b.r.append(tok)
        for b in writes:
            b.w = tok
            b.r = []
        self.lists[q].append((waits, fn, dbuf.dsem, inc))

    def fence(self):
        for e in self.lists:
            kn = self.known[e]
            for b in self.dbufs:
                key = ("d", id(b))
                if kn.get(key, 0) < b.dcnt:
                    kn[key] = b.dcnt
                    self.lists[e].append(([(b.dsem, b.dcnt)], None, None, 0))
            for o in self.CE:
                if o == e or self.cnt[o] == 0:
                    continue
                key = ("e", o)
                if kn.get(key, 0) < self.cnt[o]:
                    kn[key] = self.cnt[o]
                    self.lists[e].append(([(self.sem[o], self.cnt[o])], None, None, 0))

    def wait_dma_all(self, eng, buf):
        self.lists[eng].append(([(buf.dsem, buf.dcnt)], None, None, 0))

    def replay(self):
        nc = self.nc
        lists = self.lists
        compute_sems = list(self.sem.values())

        def run(engobj, items):
            for waits, fn, sem, inc in items:
                for s, v in waits:
                    engobj.wait_ge(s, v)
                if fn is not None:
                    ins = fn(engobj)
                    if inc == 1 and sem not in compute_sems:
                        ins.then_inc(sem)
                    else:
                        ins.then_inc(sem, inc)

        with nc.Block() as block:
            @block.tensor
            def _(e):
                run(e, lists["pe"])

            @block.scalar
            def _(e):
                run(e, lists["act"])

            @block.vector
            def _(e):
                run(e, lists["dve"])

            @block.gpsimd
            def _(e):
                run(e, lists["pool"])

            @block.sync
            def _(e):
                run(e, lists["sp"])


class Ctx:
    pass


def build_program(n_layers, mode, dbg=False):
    nc = bass.Bass("TRN2", target_bir_lowering=False)
    P = Prog(nc)
    L = n_layers

    def din(name, shape, dt=F32):
        return nc.dram_tensor(name, list(shape), dt, kind="ExternalInput").ap()

    def dout(name, shape, dt=F32):
        return nc.dram_tensor(name, list(shape), dt, kind="ExternalOutput").ap()

    def dint(name, shape, dt=F32):
        return nc.dram_tensor(name, list(shape), dt, kind="Internal").ap()

    xT0 = din("xT0", [D, TT])
    xres0 = din("xres0", [T, D])
    w_in = din("w_in", [L, D, INW])
    w_out = din("w_out", [L, D, D])
    w_kv = din("w_kv", [L, D, 1024])
    memT = din("memT", [D, NMEM])
    a_pat = din("a_pat", [L, 8, 128, 640])
    gate_w = din("gate_w", [L, 16, 256])
    gate_b = din("gate_b", [L, 128, 2])
    bnorm_g = din("bnorm_g", [L, 128, 1])
    ln_gb = din("ln_gb", [L, 2, 128, D])
    cst = din("cst", [128, 2560])
    corev = din("corev", [128, 20])
    if mode == "B":
        sx_in = din("sx_in", [NCORES, 128, 2, 130])
    out = None
    if mode in ("B", "F"):
        out = dout("out", [T, D])
    sx_out = dout("sx_out", [128, 2, 130]) if mode == "B" else None
    if mode != "B":
        sx_in = None
    dbg_y = dout("dbg_y", [D, T]) if dbg else None
    xres_s = dint("xres_s", [T, D])
    fused = (mode == "F")
    if fused:
        g_bnc = [[dint(f"g_bnc{l}_{p}", [128, 130]) for p in range(2)] for l in range(L)]
        g_gat = [[dint(f"g_gat{l}_{p}", [NCORES * 128, 130]) for p in range(2)] for l in range(L)]
        x_bnc = [dint(f"x_bnc{l}", [D, HALO], BF16) for l in range(L - 1)]
        x_gat = [dint(f"x_gat{l}", [NCORES * D, HALO], BF16) for l in range(L - 1)]
    else:
        g_bnc = g_gat = x_bnc = x_gat = None
    B_ag = Buf("allgather")
    env_xres = [Buf(f"xres_dram{i}") for i in range(8)]
    B_out = Buf("out_dram")

    def sb(name, shape, dt):
        return nc.alloc_sbuf_tensor(name, list(shape), dt)

    xT = sb("xT", [128, KT, TT], BF16)
    yT = sb("yT", [128, KT, T], BF16)
    NW = 3
    wb = [sb(f"wb{i}", [128, KT, 512], BF16) for i in range(NW)]
    wbB = [Buf(f"wb{i}") for i in range(NW)]
    corev_sb = sb("corev_sb", [128, 20], F32)
    ident_bf = sb("ident_bf", [128, 128], BF16)
    ones_bf = sb("ones_bf", [128, 128], BF16)
    B_xT, B_yT, B_cst, B_memT = Buf("xT"), Buf("yT"), Buf("cst"), Buf("memT")

    ps = [nc.alloc_psum_tensor(f"ps{i}", [128, 512], F32) for i in range(7)]
    psB = [Buf(f"ps{i}") for i in range(7)]
    pst = nc.alloc_psum_tensor("pst", [128, 1024], BF16)
    pstB = [Buf("pst0"), Buf("pst1")]
    st = Ctx()
    st.tmp_i = 0
    st.acc_i = 0
    st.w_i = 0
    st.t_i = 0

    def ps_tmp():
        i = st.tmp_i % 4
        st.tmp_i += 1
        return ps[i], psB[i]

    def ps_acc():
        i = 4 + st.acc_i % 3
        st.acc_i += 1
        return ps[i], psB[i]

    B_corev = Buf("corev")
    P.dma("sp", lambda e: e.dma_start(out=corev_sb[:], in_=corev), B_corev, writes=[B_corev])
    B_ident = Buf("ident")
    P.dma("pool", lambda e: e.dma_start(out=ident_bf[:], in_=cst[:, 2048:2176]), B_ident, writes=[B_ident])
    B_ones = Buf("ones")
    P.op("pool", lambda e: e.memset(ones_bf[:], 1.0), writes=[B_ones])
    halo_valid = corev_sb[:, 0:1]

    def load_w(src_ap, ncols):
        i = st.w_i % NW
        st.w_i += 1
        t, b = wb[i], wbB[i]
        P.dma("pool", lambda e: e.dma_start(out=t[:, :, 0:ncols], in_=src_ap.rearrange("(k p) c -> p k c", p=128)),
              b, writes=[b])
        return t, b

    for k4 in range(4):
        P.dma("pool", lambda e, k4=k4: e.dma_start(
            out=xT[:, 4 * k4:4 * k4 + 4, :],
            in_=xT0[512 * k4:512 * (k4 + 1), :].rearrange("(k p) t -> p k t", p=128)), B_xT, writes=[B_xT] if k4 == 0 else [])
    B_xT.w = ("d", B_xT, B_xT.dcnt)

    def proj_F(wt, wB, j0, ncols, tok0, ntok, extra_reads=()):
        pt, pB = ps_tmp()

        def fn(e):
            ins = None
            for kt in range(KT):
                ins = e.matmul(pt[0:ncols, 0:ntok], lhsT=wt[:, kt, j0:j0 + ncols], rhs=xT[:, kt, tok0:tok0 + ntok],
                               start=(kt == 0), stop=(kt == KT - 1))
            return ins
        P.op("pe", fn, reads=[wB, B_xT, *extra_reads], writes=[pB])
        return pt, pB

    def proj_T(wt, wB, j0, ncols, tok0, rhs_src=None, src_B=None):
        pt, pB = ps_tmp()
        src = xT if rhs_src is None else rhs_src
        sB = B_xT if src_B is None else src_B

        def fn(e):
            ins = None
            for kt in range(KT):
                ins = e.matmul(pt[:, 0:ncols], lhsT=src[:, kt, tok0:tok0 + 128], rhs=wt[:, kt, j0:j0 + ncols],
                               start=(kt == 0), stop=(kt == KT - 1))
            return ins
        P.op("pe", fn, reads=[wB, sB], writes=[pB])
        return pt, pB

    ARENA = 49152
    arena = sb("arena", [128, ARENA // 2], BF16)

    def carve(off, shape, dt):
        n = int(np.prod(shape[1:]))
        if dt == BF16:
            v = arena[:, off // 2: off // 2 + n]
        else:
            v = arena[:, off // 2: off // 2 + 2 * n].bitcast(F32)
        if len(shape) == 3:
            v = v.rearrange("p (a b) -> p a b", a=shape[1])
        return v

    ebst = sb("ebst", [128, 640], F32)
    eb = [sb(f"eb{i}", [128, 2, 640], F32) for i in range(2)]
    ebB = [Buf("eb0"), Buf("eb1")]
    B_ebst = Buf("ebst")
    ptf = [sb(f"ptf{i}", [128, 640], F32) for i in range(2)]
    ptfB = [Buf("ptf0"), Buf("ptf1")]
    ptb = [sb(f"ptb{i}", [128, 640], BF16) for i in range(3)]
    ptbB = [Buf(f"ptb{i}") for i in range(3)]
    rec = [sb(f"rec{i}", [128, 512], F32) for i in range(2)]
    recB = [Buf("rec0"), Buf("rec1")]
    st.ptf_i = st.ptb_i = st.rec_i = st.eb_i = 0
    B_lnw = Buf("lnw")
    gw_sb = sb("gw_sb", [16, 256], F32)
    gb_sb = sb("gb_sb", [128, 2], F32)
    ng_sb = sb("ng_sb", [128, 1], F32)
    B_gw = Buf("gw")
    B_mkT, B_mv = Buf("mkT"), Buf("mv")
    sxall = eb[1][:].rearrange("p a b -> p (a b)")[:, 0:NCORES * 130].rearrange("p (r c) -> p r c", r=NCORES) if fused else None
    gs = Ctx()
    gs.S_f = sb("S_f", [128, 128], F32)
    gs.ATb = [sb(f"ATb{i}", [128, 256], BF16) for i in range(2)]
    gs.gw_bf = sb("gw_bf", [16, 256], BF16)
    gs.ngb = sb("ngb", [128, 2], F32)
    gs.tot = sb("tot", [128, 2], F32)
    gs.dtmp = sb("dtmp", [128, 4], F32)
    stats = sb("stats", [128, 4, 6], F32)
    mvar = sb("mvar", [128, 4], F32)

    def normalize_gate(o_ps, oB, d_ps, dB, zs_ap, zB, y_out_ap, extra=None):
        i = st.rec_i % 2
        st.rec_i += 1
        r, rB = rec[i], recB[i]
        P.op("dve", lambda e: e.reciprocal(out=r[:], in_=d_ps[:]), reads=[dB], writes=[rB])
        P.op("dve", lambda e: e.tensor_tensor(out=r[:], in0=o_ps[:], in1=r[:], op=ALU.mult), reads=[oB, rB], writes=[rB])
        P.op("pool", lambda e: e.tensor_tensor(out=y_out_ap, in0=r[:], in1=zs_ap, op=ALU.mult),
             reads=[rB, zB], writes=[B_yT])

    for l in range(L):
        last = (l == L - 1)
        P.dma("sp", lambda e, l=l: e.dma_start(out=gw_sb[:], in_=gate_w[l]), B_gw, writes=[B_gw])
        P.dma("sp", lambda e, l=l: e.dma_start(out=gb_sb[:], in_=gate_b[l]), B_gw, writes=[])
        P.dma("sp", lambda e, l=l: e.dma_start(out=ng_sb[:], in_=bnorm_g[l]), B_gw, writes=[])
        B_gw.w = ("d", B_gw, B_gw.dcnt)

        QT = carve(0, [128, 4, T], BF16)
        KTt = carve(8192, [128, 4, TT], BF16)
        Vt = carve(20480, [128, 12, 512], BF16)
        Zs = carve(32768, [128, 4, T], F32)
        B_Q, B_K, B_V, B_Z = Buf("Q"), Buf("K"), Buf("V"), Buf("Z")
        for half in range(2):
            P.fence()
            c0 = 512 * half
            wq, wqB = load_w(w_in[l, :, C_AQ + c0:C_AQ + c0 + 512], 512)
            wk, wkB = load_w(w_in[l, :, C_AK + c0:C_AK + c0 + 512], 512)
            wv, wvB = load_w(w_in[l, :, C_AV + c0:C_AV + c0 + 512], 512)
            for h in range(4):
                for tb in range(2):
                    pt, pB = proj_F(wq, wqB, 128 * h, 128, HALO + 512 * tb, 512)
                    P.op("act", lambda e, pt=pt, h=h, tb=tb: e.activation(
                        out=QT[:, h, 512 * tb:512 * tb + 512], in_=pt[:], func=AF.Copy, scale=128.0 ** -0.5),
                        reads=[pB], writes=[B_Q])
            for h in range(4):
                for tb in range(3):
                    pt, pB = proj_F(wk, wkB, 128 * h, 128, 512 * tb, 512)
                    P.op("dve", lambda e, pt=pt, h=h, tb=tb: e.tensor_copy(
                        out=KTt[:, h, 512 * tb:512 * tb + 512], in_=pt[:]), reads=[pB], writes=[B_K])
            wz, wzB = load_w(w_in[l, :, C_AZ + c0:C_AZ + c0 + 512], 512)
            for tt in range(12):
                pt, pB = proj_T(wv, wvB, 0, 512, 128 * tt)
                eng = "act" if tt % 2 == 0 else "dve"
                if eng == "act":
                    P.op("act", lambda e, pt=pt, tt=tt: e.activation(out=Vt[:, tt, :], in_=pt[:], func=AF.Copy),
                         reads=[pB], writes=[B_V])
                else:
                    P.op("dve", lambda e, pt=pt, tt=tt: e.tensor_copy(out=Vt[:, tt, :], in_=pt[:]),
                         reads=[pB], writes=[B_V])
            for h in range(4):
                for tb in range(2):
                    pt, pB = proj_F(wz, wzB, 128 * h, 128, HALO + 512 * tb, 512)
                    P.op("act", lambda e, pt=pt, h=h, tb=tb: e.activation(
                        out=Zs[:, h, 512 * tb:512 * tb + 512], in_=pt[:], func=AF.Silu), reads=[pB], writes=[B_Z])
            for h in range(4):
                hh = 4 * half + h
                ei = st.eb_i % 2
                st.eb_i += 1
                ebt, ebtB = eb[ei], ebB[ei]
                P.dma("sp", lambda e, hh=hh, l=l: e.dma_start(out=ebst[:], in_=a_pat[l, hh]), B_ebst, writes=[B_ebst])
                P.op("act", lambda e, ebt=ebt: e.activation(out=ebt[:, 0, :], in_=ebst[:], func=AF.Exp),
                     reads=[B_ebst], writes=[ebtB])
                P.op("dve", lambda e, ebt=ebt: e.tensor_scalar(out=ebt[:, 1, :], in0=ebt[:, 0, :], scalar1=halo_valid,
                                                               scalar2=None, op0=ALU.mult),
                     reads=[ebtB, B_corev], writes=[ebtB])
                for qb in range(2):
                    o_ps, oB = ps_acc()
                    d_ps, dB = ps_acc()
                    order = [3, 4, 2, 5, 1, 6, 0, 7]
                    for oi, t in enumerate(order):
                        c_lo, c_hi = max(0, 2 * t - 8), min(8, 2 * t + 2)
                        n = (c_hi - c_lo) * 64
                        pc = (c_lo + 8 - 2 * t) * 64
                        ku = 512 * qb + 128 * t
                        q0 = 512 * qb + 64 * c_lo
                        halo_tile = ku < HALO
                        s_ps, sB = ps_tmp()
                        P.op("pe", lambda e, s_ps=s_ps, h=h, ku=ku, q0=q0, n=n: e.matmul(
                            s_ps[:, 0:n], lhsT=KTt[:, h, ku:ku + 128], rhs=QT[:, h, q0:q0 + n], start=True, stop=True),
                            reads=[B_K, B_Q], writes=[sB])
                        fi = st.ptf_i % 2
                        st.ptf_i += 1
                        pf, pfB = ptf[fi], ptfB[fi]
                        P.op("act", lambda e, s_ps=s_ps, pf=pf, n=n: e.activation(out=pf[:, 0:n], in_=s_ps[:, 0:n], func=AF.Exp),
                             reads=[sB], writes=[pfB])
                        bi = st.ptb_i % 3
                        st.ptb_i += 1
                        pb, pbB = ptb[bi], ptbB[bi]
                        P.op("dve", lambda e, pf=pf, pb=pb, n=n, pc=pc, ebt=ebt, v=(1 if halo_tile else 0): e.tensor_tensor(
                            out=pb[:, 0:n], in0=pf[:, 0:n], in1=ebt[:, v, pc:pc + n], op=ALU.mult),
                            reads=[pfB, ebtB], writes=[pbB])
                        kt_idx = ku // 128
                        P.op("pe", lambda e, o_ps=o_ps, pb=pb, n=n, c_lo=c_lo, h=h, kt_idx=kt_idx, oi=oi: e.matmul(
                            o_ps[:, 64 * c_lo:64 * c_lo + n], lhsT=Vt[:, kt_idx, 128 * h:128 * h + 128], rhs=pb[:, 0:n],
                            start=(oi == 0), stop=(oi == 7)), reads=[B_V, pbB], writes=[oB])
                        P.op("pe", lambda e, d_ps=d_ps, pb=pb, n=n, c_lo=c_lo, oi=oi: e.matmul(
                            d_ps[:, 64 * c_lo:64 * c_lo + n], lhsT=ones_bf[:], rhs=pb[:, 0:n],
                            start=(oi == 0), stop=(oi == 7)), reads=[B_ones, pbB], writes=[dB])
                    normalize_gate(o_ps, oB, d_ps, dB, Zs[:, h, 512 * qb:512 * qb + 512], B_Z,
                                   yT[:, hh, 512 * qb:512 * qb + 512])

        P.fence()
        mqT = carve(0, [128, 4, T], BF16)
        Zm = carve(8192, [128, 4, T], F32)
        memT_sb = carve(24576, [128, KT, NMEM], BF16)
        mkT = carve(32768, [128, 4, NMEM], BF16)
        mv = carve(34816, [128, 2, 512], BF16)
        B_memT = Buf("memT")
        P.dma("pool", lambda e: e.dma_start(out=memT_sb, in_=memT.rearrange("(k p) m -> p k m", p=128)),
              B_memT, writes=[B_memT])
        B_mq, B_Zm = Buf("mq"), Buf("Zm")
        wkk, wkkB = load_w(w_kv[l, :, 0:512], 512)
        wkv_, wkvB = load_w(w_kv[l, :, 512:1024], 512)
        for h in range(4):
            pt, pB = ps_tmp()

            def fn(e, pt=pt, h=h):
                ins = None
                for kt in range(KT):
                    ins = e.matmul(pt[:, 0:NMEM], lhsT=wkk[:, kt, 128 * h:128 * h + 128], rhs=memT_sb[:, kt, :],
                                   start=(kt == 0), stop=(kt == KT - 1))
                return ins
            P.op("pe", fn, reads=[wkkB, B_memT], writes=[pB])
            P.op("dve", lambda e, pt=pt, h=h: e.tensor_copy(out=mkT[:, h, :], in_=pt[:, 0:NMEM]), reads=[pB], writes=[B_mkT])
        for mt in range(2):
            pt, pB = proj_T(wkv_, wkvB, 0, 512, 128 * mt, rhs_src=memT_sb, src_B=B_memT)
            P.op("act", lambda e, pt=pt, mt=mt: e.activation(out=mv[:, mt, :], in_=pt[:], func=AF.Copy), reads=[pB], writes=[B_mv])
        wmq, wmqB = load_w(w_in[l, :, C_MQ:C_MQ + 512], 512)
        wmz, wmzB = load_w(w_in[l, :, C_MZ:C_MZ + 512], 512)
        for h in range(4):
            for tb in range(2):
                pt, pB = proj_F(wmq, wmqB, 128 * h, 128, HALO + 512 * tb, 512)
                P.op("act", lambda e, pt=pt, h=h, tb=tb: e.activation(
                    out=mqT[:, h, 512 * tb:512 * tb + 512], in_=pt[:], func=AF.Copy, scale=128.0 ** -0.5),
                    reads=[pB], writes=[B_mq])
        for h in range(4):
            for tb in range(2):
                pt, pB = proj_F(wmz, wmzB, 128 * h, 128, HALO + 512 * tb, 512)
                P.op("act", lambda e, pt=pt, h=h, tb=tb: e.activation(
                    out=Zm[:, h, 512 * tb:512 * tb + 512], in_=pt[:], func=AF.Silu), reads=[pB], writes=[B_Zm])
        for h in range(4):
            for qb in range(2):
                o_ps, oB = ps_acc()
                d_ps, dB = ps_acc()
                for mt in range(2):
                    s_ps, sB = ps_tmp()
                    P.op("pe", lambda e, s_ps=s_ps, h=h, mt=mt, qb=qb: e.matmul(
                        s_ps[:], lhsT=mkT[:, h, 128 * mt:128 * mt + 128], rhs=mqT[:, h, 512 * qb:512 * qb + 512],
                        start=True, stop=True), reads=[B_mkT, B_mq], writes=[sB])
                    bi = st.ptb_i % 3
                    st.ptb_i += 1
                    pb, pbB = ptb[bi], ptbB[bi]
                    P.op("act", lambda e, s_ps=s_ps, pb=pb: e.activation(out=pb[:, 0:512], in_=s_ps[:], func=AF.Exp),
                         reads=[sB], writes=[pbB])
                    P.op("pe", lambda e, o_ps=o_ps, pb=pb, h=h, mt=mt: e.matmul(
                        o_ps[:], lhsT=mv[:, mt, 128 * h:128 * h + 128], rhs=pb[:, 0:512], start=(mt == 0), stop=(mt == 1)),
                        reads=[B_mv, pbB], writes=[oB])
                    P.op("pe", lambda e, d_ps=d_ps, pb=pb, mt=mt: e.matmul(
                        d_ps[:], lhsT=ones_bf[:], rhs=pb[:, 0:512], start=(mt == 0), stop=(mt == 1)),
                        reads=[B_ones, pbB], writes=[dB])
                normalize_gate(o_ps, oB, d_ps, dB, Zm[:, h, 512 * qb:512 * qb + 512], B_Zm,
                               yT[:, 12 + h, 512 * qb:512 * qb + 512])

        P.fence()
        gla_group(nc, P, st, l, locals())

        P.fence()
        if dbg and l == L - 1:
            ydb = carve(0, [128, T], F32)
            B_ydb = Buf("ydb")
            for kt in range(KT):
                P.op("dve", lambda e, kt=kt: e.tensor_copy(out=ydb, in_=yT[:, kt, :]), reads=[B_yT], writes=[B_ydb])
                P.dma("sp", lambda e, kt=kt: e.dma_start(out=dbg_y[128 * kt:128 * kt + 128, :], in_=ydb), B_ydb, reads=[B_ydb])
            P.fence()
        xo = carve(0, [128, 4, D], F32)
        xoB = [Buf(f"xo{i}") for i in range(4)]
        lnw = carve(32768, [128, 2, D], F32)
        P.dma("sp", lambda e, l=l: e.dma_start(out=lnw, in_=ln_gb[l].rearrange("a p d -> p a d")), B_lnw, writes=[B_lnw])
        B_stats = Buf("stats")
        B_xres = env_xres
        for hp in range(2):
            src = xres0 if l == 0 else xres_s
            for t4 in range(4):
                tt = 4 * hp + t4
                P.dma("sp", lambda e, tt=tt, t4=t4, src=src: e.dma_start(out=xo[:, t4, :], in_=src[128 * tt:128 * tt + 128, :]),
                      xoB[t4], reads=[B_xres[tt]], writes=[xoB[t4]])
            for cb in range(4):
                wt, wB = load_w(w_out[l, :, 512 * cb:512 * cb + 512], 512)
                for t4 in range(4):
                    tt = 4 * hp + t4
                    pt, pB = proj_T(wt, wB, 0, 512, 128 * tt, rhs_src=yT, src_B=B_yT)
                    P.op("dve", lambda e, pt=pt, t4=t4, cb=cb: e.scalar_tensor_tensor(
                        out=xo[:, t4, 512 * cb:512 * cb + 512], in0=xo[:, t4, 512 * cb:512 * cb + 512], scalar=ALPHA,
                        in1=pt[:], op0=ALU.mult, op1=ALU.add), reads=[pB], writes=[xoB[t4]])
            for t4 in range(4):
                tt = 4 * hp + t4
                xB = xoB[t4]
                for cb in range(4):
                    P.op("dve", lambda e, cb=cb, t4=t4: e.bn_stats(out=stats[:, cb, :], in_=xo[:, t4, 512 * cb:512 * cb + 512]),
                         reads=[xB], writes=[B_stats])
                P.op("dve", lambda e: e.bn_aggr(out=mvar[:, 0:2], in_=stats[:]), reads=[B_stats], writes=[B_stats])
                P.op("dve", lambda e: e.tensor_scalar(out=mvar[:, 3:4], in0=mvar[:, 1:2], scalar1=LN_EPS, scalar2=None,
                                                      op0=ALU.add), reads=[B_stats], writes=[B_stats])
                P.op("act", lambda e: e.activation(out=mvar[:, 3:4], in_=mvar[:, 3:4], func=AF.Sqrt),
                     reads=[B_stats], writes=[B_stats])
                P.op("dve", lambda e: e.reciprocal(out=mvar[:, 2:3], in_=mvar[:, 3:4]), reads=[B_stats], writes=[B_stats])
                P.op("dve", lambda e, t4=t4: e.tensor_scalar(out=xo[:, t4, :], in0=xo[:, t4, :], scalar1=mvar[:, 0:1],
                                                             scalar2=mvar[:, 2:3], op0=ALU.subtract, op1=ALU.mult),
                     reads=[B_stats, xB], writes=[xB])
                P.op("pool", lambda e, t4=t4: e.tensor_tensor(out=xo[:, t4, :], in0=xo[:, t4, :], in1=lnw[:, 0, :], op=ALU.mult),
                     reads=[xB, B_lnw], writes=[xB])
                P.op("dve", lambda e, t4=t4: e.tensor_tensor(out=xo[:, t4, :], in0=xo[:, t4, :], in1=lnw[:, 1, :], op=ALU.add),
                     reads=[xB, B_lnw], writes=[xB])
                if last:
                    P.dma("sp", lambda e, tt=tt, t4=t4: e.dma_start(out=out[128 * tt:128 * tt + 128, :], in_=xo[:, t4, :]),
                          xB, reads=[xB])
                else:
                    P.dma("sp", lambda e, tt=tt, t4=t4: e.dma_start(out=xres_s[128 * tt:128 * tt + 128, :], in_=xo[:, t4, :]),
                          xB, reads=[xB], writes=[B_xres[tt]])
                    if fused:
                        xn = eb[0][:].rearrange("p a b -> p (a b)").bitcast(BF16)[:, 0:D]
                        P.op("act", lambda e, t4=t4, xn=xn: e.activation(out=xn, in_=xo[:, t4, :], func=AF.Copy), reads=[xB], writes=[ebB[0]])
                        for k8 in range(2):
                            def fnT(e, k8=k8, xn=xn):
                                ins = None
                                for kk in range(8):
                                    kt = 8 * k8 + kk
                                    ins = e.transpose(out=pst[:, 128 * kk:128 * kk + 128], in_=xn[:, 128 * kt:128 * kt + 128],
                                                      identity=ident_bf[:])
                                return ins
                            P.op("pe", fnT, reads=[ebB[0], B_ident], writes=[pstB[0]])
                            P.op("dve", lambda e, k8=k8, tt=tt: e.tensor_copy(
                                out=xT[:, 8 * k8:8 * k8 + 8, HALO + 128 * tt:HALO + 128 * tt + 128],
                                in_=pst[:, :].rearrange("p (a b) -> p a b", a=8)), reads=[pstB[0]], writes=[B_xT])
        if fused and not last:
            B_xb, B_xg = Buf("x_bnc"), Buf("x_gat")
            P.dma("sp", lambda e, l=l: e.dma_start(out=x_bnc[l].rearrange("(k p) t -> p k t", p=128), in_=xT[:, :, HALO + 512:HALO + 1024]),
                  B_xT, reads=[B_xT], writes=[B_xb])
            P.dma("pool", lambda e, l=l: e.collective_compute(
                "AllGather", ALU.bypass, replica_groups=[list(range(NCORES))], ins=[x_bnc[l].opt()], outs=[x_gat[l].opt()]),
                B_ag, reads=[B_xb], writes=[B_xg], inc=1)
            for r in range(NCORES - 1):
                i = st.w_i % NW
                st.w_i += 1
                ht, hB = wb[i], wbB[i]
                P.dma("sp", lambda e, r=r, l=l, ht=ht: e.dma_start(
                    out=ht[:], in_=x_gat[l][D * r:D * (r + 1), :].rearrange("(k p) t -> p k t", p=128)), hB, reads=[B_xg], writes=[hB])
                oh = corev_sb[:, 1 + r:2 + r]
                if r == 0:
                    P.op("dve", lambda e, ht=ht, oh=oh: e.tensor_scalar(out=xT[:, :, 0:HALO], in0=ht[:], scalar1=oh, scalar2=None,
                                                                       op0=ALU.mult), reads=[hB, B_corev], writes=[B_xT])
                else:
                    P.op("dve", lambda e, ht=ht, oh=oh: e.scalar_tensor_tensor(
                        out=xT[:, :, 0:HALO], in0=ht[:], scalar=oh, in1=xT[:, :, 0:HALO], op0=ALU.mult, op1=ALU.add),
                        reads=[hB, B_corev, B_xT], writes=[B_xT])
        if last:
            for xB in xoB:
                P.wait_dma_all("sp", xB)
            if sx_out is not None:
                P.wait_dma_all("sp", st.B_sxo)
    P.replay()
    return nc


def gla_group(nc, P, st, l, env):
    g = env
    carve, load_w, proj_F, proj_T = g["carve"], g["load_w"], g["proj_F"], g["proj_T"]
    yT, B_yT, w_in, cst = g["yT"], g["B_yT"], g["w_in"], g["cst"]
    corev_sb, B_corev = g["corev_sb"], g["B_corev"]
    ps_tmp, ps_acc, pst, pstB = g["ps_tmp"], g["ps_acc"], g["pst"], g["pstB"]
    ident_bf, B_ident, ones_bf, B_ones = g["ident_bf"], g["B_ident"], g["ones_bf"], g["B_ones"]
    gw_sb, gb_sb, ng_sb, B_gw = g["gw_sb"], g["gb_sb"], g["ng_sb"], g["B_gw"]
    sx_in, sx_out, gs = g["sx_in"], g["sx_out"], g["gs"]
    ptf, ptb, rec, ebst = g["ptf"], g["ptb"], g["rec"], g["ebst"]

    bv = carve(0, [128, 8, 512], BF16)
    rmask = carve(8192, [128, T], F32)
    Mfull = carve(12288, [128, 512], F32)
    blr_bf = carve(14336, [128, T], BF16)
    sp = carve(16384, [128, T], F32)
    bs = carve(20480, [128, T], F32)
    ep = carve(24576, [128, T], F32)
    em = carve(28672, [128, T], F32)
    qp = carve(32768, [128, T], BF16)
    qm = carve(34816, [128, T], BF16)
    kpz = [carve(36864, [128, T], BF16), carve(22528, [128, T], BF16)]
    kmz = [carve(38912, [128, T], BF16), carve(20480, [128, T], BF16)]
    kdT = carve(40960, [128, T], BF16)
    kdz = [carve(43008, [128, 8, 128], BF16), carve(47104, [128, 8, 128], BF16)]
    ATm = carve(45056, [128, 512], F32)
    Zb = [carve(28672, [128, 512], F32), ptf[0][:, 0:512]]
    hmask = [corev_sb[:, 16:17], corev_sb[:, 17:18]]
    S_z = [[ptb[1 + i][:, 128 * hh:128 * hh + 128] for hh in range(2)] for i in range(2)]
    o_blk = [rec[0][:], rec[1][:]]
    osq = ptb[0][:, 0:512]
    rst = ptf[1][:, 0:512]
    sxr = ebst[:, 0:260].rearrange("p (a b) -> p a b", a=2)
    sxo = ebst[:, 260:520].rearrange("p (a b) -> p a b", a=2)
    S_f, ATb, gw_bf, ngb, tot, dtmp = gs.S_f, gs.ATb, gs.gw_bf, gs.ngb, gs.tot, gs.dtmp
    Bn = lambda n: Buf(n)
    B_bv, B_rm, B_Mf, B_blr, B_sp, B_bs, B_ep, B_em = (Bn(x) for x in ("bv", "rm", "Mf", "blr", "sp", "bs", "ep", "em"))
    B_qp, B_qm, B_kp, B_km, B_kdT, B_kdtm, B_ATm = (Bn(x) for x in ("qp", "qm", "kp", "km", "kdT", "kdtm", "ATm"))
    B_Zb = [B_em, Bn("Zb1")]
    B_kpz = [B_kp, B_bs]
    B_kmz = [B_km, B_bs]
    B_ob = [Bn("ob0"), Bn("ob1")]
    B_osq, B_rst, B_sxr, B_sxo, B_Sf = Bn("osq"), Bn("rst"), Bn("sxr"), Bn("sxo"), Bn("Sf")
    B_Sbf = [Bn("Sbf0"), Bn("Sbf1")]
    B_ATb = [Bn("ATb0"), Bn("ATb1")]
    B_gwbf, B_small = Bn("gwbf"), Bn("small")

    P.dma("sp", lambda e: e.dma_start(out=rmask, in_=cst[:, 0:1024]), B_rm, writes=[B_rm])
    P.dma("sp", lambda e: e.dma_start(out=Mfull, in_=cst[:, 1024:1536]), B_Mf, writes=[B_Mf])
    P.op("dve", lambda e: e.tensor_copy(out=gw_bf[:], in_=gw_sb[:]), reads=[B_gw], writes=[B_gwbf])
    P.op("dve", lambda e: e.tensor_scalar(out=ngb[:], in0=gb_sb[:], scalar1=-1.0, scalar2=None, op0=ALU.mult),
         reads=[B_gw], writes=[B_small])

    wlr, wlrB = load_w(w_in[l, :, C_BLR:C_BLR + 16], 16)
    for tb in range(2):
        pt, pB = proj_F(wlr, wlrB, 0, 16, HALO + 512 * tb, 512)
        P.op("act", lambda e, pt=pt, tb=tb: e.activation(out=blr_bf[0:16, 512 * tb:512 * tb + 512], in_=pt[0:16, :], func=AF.Copy),
             reads=[pB], writes=[B_blr])
    wv, wvB = load_w(w_in[l, :, C_BV:C_BV + 512], 512)
    for tt in range(8):
        pt, pB = proj_T(wv, wvB, 0, 512, HALO + 128 * tt)
        P.op("act", lambda e, pt=pt, tt=tt: e.activation(out=bv[:, tt, :], in_=pt[:], func=AF.Copy), reads=[pB], writes=[B_bv])
    wqk, wqkB = load_w(w_in[l, :, C_BQ:C_BQ + 512], 512)
    wz, wzB = load_w(w_in[l, :, C_BZ:C_BZ + 512], 512)

    for p in range(2):
        for tb in range(2):
            pt, pB = ps_tmp()
            P.op("pe", lambda e, pt=pt, tb=tb, p=p: e.matmul(pt[:], lhsT=gw_bf[0:16, 128 * p:128 * p + 128],
                                                          rhs=blr_bf[0:16, 512 * tb:512 * tb + 512], start=True, stop=True),
                 reads=[B_gwbf, B_blr], writes=[pB])
            P.op("act", lambda e, pt=pt, tb=tb, p=p: e.activation(out=sp[:, 512 * tb:512 * tb + 512], in_=pt[:], func=AF.Exp,
                                                               bias=ngb[:, p:p + 1], scale=-1.0),
                 reads=[pB, B_small], writes=[B_sp])
        P.op("act", lambda e: e.activation(out=sp, in_=sp, func=AF.Ln, bias=1.0, scale=1.0), reads=[B_sp], writes=[B_sp])
        P.op("dve", lambda e: e.tensor_tensor_scan(out=bs, data0=rmask, data1=sp, initial=0.0, op0=ALU.mult, op1=ALU.add),
             reads=[B_sp, B_rm], writes=[B_bs])
        P.op("dve", lambda e, p=p: e.reduce_sum(out=tot[:, p:p + 1], in_=sp, axis=mybir.AxisListType.X),
             reads=[B_sp], writes=[B_small])
        P.op("act", lambda e, p=p: e.activation(out=sxo[:, p, 128:129], in_=tot[:, p:p + 1], func=AF.Exp, scale=-1.0 / 16),
             reads=[B_small], writes=[B_sxo])
        P.op("act", lambda e: e.activation(out=ep, in_=bs, func=AF.Exp, scale=-1.0 / 16), reads=[B_bs], writes=[B_ep])
        P.op("act", lambda e: e.activation(out=em, in_=bs, func=AF.Exp, scale=1.0 / 16), reads=[B_bs], writes=[B_em])
        P.op("dve", lambda e: e.tensor_tensor(
            out=sp.rearrange("p (n c) -> p n c", c=64), in0=bs.rearrange("p (n c) -> p n c", c=64),
            in1=bs[:, 63::64].unsqueeze(2).broadcast_to([128, 16, 64]), op=ALU.subtract), reads=[B_bs, B_sp], writes=[B_sp])
        P.op("act", lambda e: e.activation(out=sp, in_=sp, func=AF.Exp, scale=1.0 / 16), reads=[B_sp], writes=[B_sp])
        for tb in range(2):
            cs = slice(512 * tb, 512 * tb + 512)
            pt, pB = proj_F(wqk, wqkB, 128 * p, 128, HALO + 512 * tb, 512)
            P.op("dve", lambda e, pt=pt, cs=cs: e.scalar_tensor_tensor(out=qp[:, cs], in0=pt[:], scalar=0.125, in1=ep[:, cs],
                                                                     op0=ALU.mult, op1=ALU.mult), reads=[pB, B_ep], writes=[B_qp])
            P.op("dve", lambda e, pt=pt, cs=cs: e.scalar_tensor_tensor(out=qm[:, cs], in0=pt[:], scalar=0.125, in1=em[:, cs],
                                                                     op0=ALU.mult, op1=ALU.mult), reads=[pB, B_em], writes=[B_qm])
            pt, pB = proj_F(wqk, wqkB, 256 + 128 * p, 128, HALO + 512 * tb, 512)
            for hh in range(2):
                P.op("dve", lambda e, pt=pt, cs=cs, hh=hh: e.scalar_tensor_tensor(
                    out=kpz[hh][:, cs], in0=pt[:], scalar=hmask[hh], in1=ep[:, cs], op0=ALU.mult, op1=ALU.mult),
                    reads=[pB, B_ep, B_corev], writes=[B_kpz[hh]])
                P.op("dve", lambda e, pt=pt, cs=cs, hh=hh: e.scalar_tensor_tensor(
                    out=kmz[hh][:, cs], in0=pt[:], scalar=hmask[hh], in1=em[:, cs], op0=ALU.mult, op1=ALU.mult),
                    reads=[pB, B_em, B_corev], writes=[B_kmz[hh]])
            P.op("dve", lambda e, pt=pt, cs=cs: e.tensor_tensor(out=kdT[:, cs], in0=pt[:], in1=sp[:, cs], op=ALU.mult),
                 reads=[pB, B_sp], writes=[B_kdT])
        if GLA_STOP == 1:
            continue
        for tt in range(8):
            tp, tB = pst[:, 0:128], pstB[0]
            P.op("pe", lambda e, tp=tp, tt=tt: e.transpose(out=tp, in_=kdT[:, 128 * tt:128 * tt + 128], identity=ident_bf[:]),
                 reads=[B_kdT, B_ident], writes=[tB])
            for cc in range(2):
                P.op("act", lambda e, tp=tp, tt=tt, cc=cc: e.activation(out=kdz[cc][:, tt, :], in_=tp, func=AF.Copy, scale=hmask[cc]),
                     reads=[tB, B_corev], writes=[B_kdtm])
        P.op("pool", lambda e: e.memset(S_f[:], 0.0), writes=[B_Sf])
        if g["fused"]:
            for n in range(16):
                tt, cc = n // 2, n % 2
                u_ps, uB = ps_tmp()
                P.op("pe", lambda e, u_ps=u_ps, cc=cc, tt=tt, p=p: e.matmul(
                    u_ps[:, 0:256], lhsT=kdz[cc][:, tt, :], rhs=bv[:, tt, 256 * p:256 * p + 256],
                    start=True, stop=True), reads=[B_kdtm, B_bv], writes=[uB])
                P.op("dve", lambda e, u_ps=u_ps: e.tensor_scalar(out=ATm[:, 0:128], in0=u_ps[:, 0:128], scalar1=hmask[0],
                                                                 scalar2=None, op0=ALU.mult), reads=[uB, B_corev], writes=[B_ATm])
                P.op("dve", lambda e, u_ps=u_ps: e.scalar_tensor_tensor(
                    out=ATm[:, 0:128], in0=u_ps[:, 128:256], scalar=hmask[1], in1=ATm[:, 0:128], op0=ALU.mult, op1=ALU.add),
                    reads=[uB, B_corev, B_ATm], writes=[B_ATm])
                P.op("dve", lambda e, n=n: e.scalar_tensor_tensor(
                    out=S_f[:], in0=S_f[:], scalar=ep[:, 64 * n + 63:64 * n + 64], in1=ATm[:, 0:128],
                    op0=ALU.mult, op1=ALU.add), reads=[B_ATm, B_ep, B_Sf], writes=[B_Sf])
            P.op("dve", lambda e, p=p: e.tensor_copy(out=sxo[:, p, 0:128], in_=S_f[:]), reads=[B_Sf], writes=[B_sxo])
            bnc, gat = g["g_bnc"][l][p], g["g_gat"][l][p]
            B_bnc, B_gat, B_ag = Bn("bnc"), Bn("gat"), g["B_ag"]
            P.dma("sp", lambda e, p=p, bnc=bnc: e.dma_start(out=bnc, in_=sxo[:, p, :]), B_sxo, reads=[B_sxo], writes=[B_bnc])
            P.dma("pool", lambda e, bnc=bnc, gat=gat: e.collective_compute(
                "AllGather", ALU.bypass, replica_groups=[list(range(NCORES))], ins=[bnc.opt()], outs=[gat.opt()]),
                B_ag, reads=[B_bnc], writes=[B_gat], inc=1)
            sxall = g["sxall"]
            B_sxall = Bn("sxall")
            P.dma("sp", lambda e, gat=gat: e.dma_start(out=sxall, in_=gat.rearrange("(r q) c -> q r c", q=128)),
                  B_sxall, reads=[B_gat], writes=[B_sxall])
            P.op("pool", lambda e: e.memset(S_f[:], 0.0), reads=[B_sxo], writes=[B_Sf])
            for r in range(NCORES - 1):
                m_ap = corev_sb[:, 8 + r:9 + r]
                P.op("dve", lambda e, r=r, m_ap=m_ap: e.tensor_scalar(out=dtmp[:, 0:1], in0=sxall[:, r, 128:129], scalar1=-1.0,
                                                                   scalar2=m_ap, op0=ALU.add, op1=ALU.mult),
                     reads=[B_sxall, B_corev], writes=[B_small])
                P.op("dve", lambda e: e.tensor_scalar(out=dtmp[:, 0:1], in0=dtmp[:, 0:1], scalar1=1.0, scalar2=None, op0=ALU.add),
                     reads=[B_small], writes=[B_small])
                P.op("dve", lambda e, r=r, m_ap=m_ap: e.tensor_scalar(out=ATm[:, 0:128], in0=sxall[:, r, 0:128], scalar1=m_ap,
                                                                   scalar2=None, op0=ALU.mult),
                     reads=[B_sxall, B_corev], writes=[B_ATm])
                P.op("dve", lambda e: e.scalar_tensor_tensor(out=S_f[:], in0=S_f[:], scalar=dtmp[:, 0:1], in1=ATm[:, 0:128],
                                                             op0=ALU.mult, op1=ALU.add), reads=[B_small, B_ATm, B_Sf], writes=[B_Sf])
        elif sx_in is not None:
            for r in range(NCORES - 1):
                P.dma("sp", lambda e, r=r: e.dma_start(out=sxr, in_=sx_in[r]), B_sxr, writes=[B_sxr])
                m_ap = corev_sb[:, 8 + r:9 + r]
                P.op("dve", lambda e, p=p, m_ap=m_ap: e.tensor_scalar(out=dtmp[:, 0:1], in0=sxr[:, p, 128:129], scalar1=-1.0,
                                                                   scalar2=m_ap, op0=ALU.add, op1=ALU.mult),
                     reads=[B_sxr, B_corev], writes=[B_small])
                P.op("dve", lambda e: e.tensor_scalar(out=dtmp[:, 0:1], in0=dtmp[:, 0:1], scalar1=1.0, scalar2=None, op0=ALU.add),
                     reads=[B_small], writes=[B_small])
                P.op("dve", lambda e, p=p, m_ap=m_ap: e.tensor_scalar(out=ATm[:, 0:128], in0=sxr[:, p, 0:128], scalar1=m_ap,
                                                                   scalar2=None, op0=ALU.mult),
                     reads=[B_sxr, B_corev], writes=[B_ATm])
                P.op("dve", lambda e: e.scalar_tensor_tensor(out=S_f[:], in0=S_f[:], scalar=dtmp[:, 0:1], in1=ATm[:, 0:128],
                                                             op0=ALU.mult, op1=ALU.add), reads=[B_small, B_ATm, B_Sf], writes=[B_Sf])
        sbi = 0
        for hh in range(2):
            P.op("act", lambda e, hh=hh: e.activation(out=S_z[0][hh], in_=S_f[:], func=AF.Copy, scale=hmask[hh]),
                 reads=[B_Sf, B_corev], writes=[B_Sbf[0]])
        if GLA_STOP == 2:
            continue
        for tt in range(8):
            tb, t4 = tt // 4, tt % 4
            tsl = slice(128 * tt, 128 * tt + 128)
            if t4 == 0:
                for hh in range(2):
                    pt, pB = proj_F(wz, wzB, 128 * (2 * p + hh), 128, HALO + 512 * tb, 512)
                    P.op("act", lambda e, pt=pt, hh=hh: e.activation(out=Zb[hh], in_=pt[:], func=AF.Silu), reads=[pB], writes=[B_Zb[hh]])
            a_ps, aB = ps_tmp()

            def fn(e, a_ps=a_ps, tsl=tsl):
                ins = None
                for hh in range(2):
                    e.matmul(a_ps[:, 128 * hh:128 * hh + 128], lhsT=kmz[hh][:, tsl], rhs=qp[:, tsl], start=True, stop=True)
                    ins = e.matmul(a_ps[:, 256 + 128 * hh:256 + 128 * hh + 128], lhsT=kpz[hh][:, tsl], rhs=qm[:, tsl],
                                   start=True, stop=True)
                return ins
            P.op("pe", fn, reads=[B_km, B_qp, B_kp, B_qm, B_bs], writes=[aB])
            P.op("dve", lambda e, a_ps=a_ps: e.tensor_tensor(out=ATm, in0=a_ps[:], in1=Mfull, op=ALU.mult),
                 reads=[aB, B_Mf], writes=[B_ATm])
            ai = tt % 2
            P.op("pool", lambda e, ai=ai: e.tensor_tensor(out=ATb[ai][:], in0=ATm[:, 0:256], in1=ATm[:, 256:512], op=ALU.add),
                 reads=[B_ATm], writes=[B_ATb[ai]])
            if GLA_STOP == 3:
                continue
            o_ps = [ps_acc(), ps_acc()]
            for hh in range(2):
                P.op("pe", lambda e, hh=hh, ai=ai, tt=tt, op_=o_ps[hh][0], p=p: e.matmul(
                    op_[:, 0:128], lhsT=bv[:, tt, 128 * (2 * p + hh):128 * (2 * p + hh) + 128], rhs=ATb[ai][:, 128 * hh:128 * hh + 128],
                    start=True, stop=False), reads=[B_bv, B_ATb[ai]], writes=[o_ps[hh][1]])
            for cc in range(2):
                n = 2 * tt + cc
                csl = slice(64 * n, 64 * n + 64)
                for hh in range(2):
                    P.op("pe", lambda e, hh=hh, csl=csl, cc=cc, sbi=sbi, op_=o_ps[hh][0]: e.matmul(
                        op_[:, 64 * cc:64 * cc + 64], lhsT=S_z[sbi][hh], rhs=qp[:, csl], start=False, stop=(cc == 1)),
                        reads=[B_Sbf[sbi], B_qp], writes=[o_ps[hh][1]])
                u_ps, uB = ps_tmp()
                P.op("pe", lambda e, u_ps=u_ps, cc=cc, tt=tt, p=p: e.matmul(
                    u_ps[:, 0:256], lhsT=kdz[cc][:, tt, :], rhs=bv[:, tt, 256 * p:256 * p + 256],
                    start=True, stop=True), reads=[B_kdtm, B_bv], writes=[uB])
                P.op("dve", lambda e, u_ps=u_ps: e.tensor_scalar(out=ATm[:, 0:128], in0=u_ps[:, 0:128], scalar1=hmask[0],
                                                                 scalar2=None, op0=ALU.mult), reads=[uB, B_corev], writes=[B_ATm])
                P.op("dve", lambda e, u_ps=u_ps: e.scalar_tensor_tensor(
                    out=ATm[:, 0:128], in0=u_ps[:, 128:256], scalar=hmask[1], in1=ATm[:, 0:128], op0=ALU.mult, op1=ALU.add),
                    reads=[uB, B_corev, B_ATm], writes=[B_ATm])
                P.op("dve", lambda e, n=n: e.scalar_tensor_tensor(
                    out=S_f[:], in0=S_f[:], scalar=ep[:, 64 * n + 63:64 * n + 64], in1=ATm[:, 0:128],
                    op0=ALU.mult, op1=ALU.add), reads=[B_ATm, B_ep, B_Sf], writes=[B_Sf])
                sbi = 1 - sbi
                for hh in range(2):
                    P.op("act", lambda e, sbi=sbi, hh=hh: e.activation(out=S_z[sbi][hh], in_=S_f[:], func=AF.Copy, scale=hmask[hh]),
                         reads=[B_Sf, B_corev], writes=[B_Sbf[sbi]])
            for hh in range(2):
                P.op("act", lambda e, hh=hh, t4=t4, op_=o_ps[hh][0]: e.activation(
                    out=o_blk[hh][:, 128 * t4:128 * t4 + 128], in_=op_[:, 0:128], func=AF.Copy),
                    reads=[o_ps[hh][1]], writes=[B_ob[hh]])
            if t4 == 3:
                for hh in range(2):
                    P.op("act", lambda e, hh=hh: e.activation(out=osq, in_=o_blk[hh], func=AF.Square), reads=[B_ob[hh]], writes=[B_osq])
                    m_ps, mB = ps_tmp()
                    P.op("pe", lambda e, m_ps=m_ps: e.matmul(m_ps[:], lhsT=ones_bf[:], rhs=osq, start=True, stop=True),
                         reads=[B_ones, B_osq], writes=[mB])
                    P.op("dve", lambda e, m_ps=m_ps: e.tensor_scalar(out=rst, in0=m_ps[:], scalar1=1.0 / 128, scalar2=RMS_EPS,
                                                                   op0=ALU.mult, op1=ALU.add), reads=[mB], writes=[B_rst])
                    P.op("act", lambda e: e.activation(out=rst, in_=rst, func=AF.Sqrt), reads=[B_rst], writes=[B_rst])
                    P.op("dve", lambda e: e.reciprocal(out=rst, in_=rst), reads=[B_rst], writes=[B_rst])
                    P.op("dve", lambda e, hh=hh: e.scalar_tensor_tensor(out=rst, in0=rst, scalar=ng_sb[:, 0:1], in1=o_blk[hh],
                                                                      op0=ALU.mult, op1=ALU.mult),
                         reads=[B_rst, B_gw, B_ob[hh]], writes=[B_rst])
                    P.op("pool", lambda e, hh=hh, tb=tb, p=p: e.tensor_tensor(
                        out=yT[:, 8 + 2 * p + hh, 512 * tb:512 * tb + 512], in0=rst, in1=Zb[hh], op=ALU.mult),
                        reads=[B_rst, B_Zb[hh]], writes=[B_yT])
        P.op("dve", lambda e, p=p: e.tensor_copy(out=sxo[:, p, 0:128], in_=S_f[:]), reads=[B_Sf], writes=[B_sxo])
    if GLA_STOP:
        P.op("pool", lambda e: e.memset(yT[:, 8:12, :], 0.0), writes=[B_yT])
        P.op("pool", lambda e: e.memset(sxo, 0.0), writes=[B_sxo])
    if sx_out is not None:
        P.dma("sp", lambda e: e.dma_start(out=sx_out, in_=sxo), B_sxo, reads=[B_sxo])
        st.B_sxo = B_sxo


def _consts():
    c = np.zeros((128, 2560), np.float32)
    t = np.arange(1024)
    c[:, 0:1024] = (t % 64 != 0).astype(np.float32)[None, :]
    j = np.arange(128)[:, None]
    i = np.arange(128)[None, :]
    same = (j // 64) == (i // 64)
    m1 = (same & (i >= j)).astype(np.float32)
    m2 = (same & (i < j)).astype(np.float32)
    c[:, 1024:1280] = np.tile(m1, (1, 2))
    c[:, 1280:1536] = np.tile(m2, (1, 2))
    c[:, 2048:2176] = np.eye(128, dtype=np.float32)
    return c


def _bias_pattern(tab):
    k = np.arange(128)[:, None]
    jj = np.arange(640)[None, :]
    dist = jj - k
    idx = np.clip(dist, -128, 128) + 128
    cp = jj // 64
    kc = (k >= 64).astype(np.int64)
    valid = (cp >= kc) & (cp <= 8 + kc)
    pat = tab[:, :, idx]
    pat = np.where(valid[None, None], pat, np.float32(NEG)).astype(np.float32)
    return np.ascontiguousarray(pat)


def prep_common(inp, l0, l1):
    L = l1 - l0
    cm = {}
    cm["w_in"] = np.ascontiguousarray(inp["w_in"][l0:l1])
    cm["w_out"] = np.ascontiguousarray(inp["w_out"][l0:l1])
    cm["w_kv"] = np.ascontiguousarray(inp["w_mem_kv"][l0:l1])
    cm["memT"] = np.ascontiguousarray(inp["mem"][0].T)
    cm["a_pat"] = _bias_pattern(np.asarray(inp["a_rel_bias"][l0:l1]))
    cm["gate_w"] = np.ascontiguousarray(inp["b_gate_w"][l0:l1])
    cm["gate_b"] = np.ascontiguousarray(np.asarray(inp["b_gate_b"][l0:l1]).reshape(L, 2, 128).transpose(0, 2, 1))
    cm["bnorm_g"] = np.ascontiguousarray(np.asarray(inp["b_norm_g"][l0:l1]).reshape(L, 128, 1))
    g = np.asarray(inp["ln_g"][l0:l1])
    b = np.asarray(inp["ln_b"][l0:l1])
    gb = np.stack([g, b], axis=1)
    cm["ln_gb"] = np.ascontiguousarray(np.broadcast_to(gb[:, :, None, :], (L, 2, 128, D)))
    cm["cst"] = _consts()
    return cm


def prep_core(x_full, c):
    m = {}
    xt = np.zeros((D, TT), np.float32)
    lo = c * T - HALO
    if c > 0:
        xt[:, :HALO] = x_full[lo:c * T].T
    xt[:, HALO:] = x_full[c * T:(c + 1) * T].T
    m["xT0"] = xt
    m["xres0"] = np.ascontiguousarray(x_full[c * T:(c + 1) * T])
    cv = np.zeros((128, 20), np.float32)
    cv[0:64, 16] = 1.0
    cv[64:128, 17] = 1.0
    cv[:, 0] = 0.0 if c == 0 else 1.0
    for r in range(7):
        cv[:, 1 + r] = 1.0 if r == c - 1 else 0.0
    for r in range(8):
        cv[:, 8 + r] = 1.0 if r < c else 0.0
    m["corev"] = cv
    return m


def kernel(x, mem, w_in, a_rel_bias, b_gate_w, b_gate_b, b_norm_g, w_mem_kv, w_out, ln_g, ln_b):
    inp = dict(x=np.asarray(x), mem=np.asarray(mem), w_in=np.asarray(w_in), a_rel_bias=np.asarray(a_rel_bias),
               b_gate_w=np.asarray(b_gate_w), b_gate_b=np.asarray(b_gate_b), b_norm_g=np.asarray(b_norm_g),
               w_mem_kv=np.asarray(w_mem_kv), w_out=np.asarray(w_out), ln_g=np.asarray(ln_g), ln_b=np.asarray(ln_b))
    x_full = np.ascontiguousarray(inp["x"][0], dtype=np.float32)
    if FUSED:
        nc = build_program(DEPTH, "F")
        cm = prep_common(inp, 0, DEPTH)
        maps = []
        for c in range(NCORES):
            m = dict(cm)
            m.update(prep_core(x_full, c))
            maps.append(m)
        res = run_bass_kernel_spmd(nc, maps, core_ids=list(range(NCORES)))
        out = np.concatenate([res.results[c]["out"] for c in range(NCORES)], axis=0)
        return np.ascontiguousarray(out[None], dtype=np.float32)
    for l in range(DEPTH):
        cm = prep_common(inp, l, l + 1)
        cores = [prep_core(x_full, c) for c in range(NCORES)]
        sx = np.zeros((NCORES, 128, 2, 130), np.float32)
        res = None
        for pas in range(2):
            nc = build_program(1, "B")
            maps = []
            for c in range(NCORES):
                m = dict(cm)
                m.update(cores[c])
                m["sx_in"] = sx
                maps.append(m)
            res = run_bass_kernel_spmd(nc, maps, core_ids=list(range(NCORES)))
            if pas == 0:
                sx = np.ascontiguousarray(np.stack([res.results[c]["sx_out"] for c in range(NCORES)], axis=0))
        x_full = np.ascontiguousarray(np.concatenate([res.results[c]["out"] for c in range(NCORES)], axis=0))
    return x_full[None].astype(np.float32)
```
